# Optimizing a Trainium2 kernel written in Bass

```python
import math
import jax, jax.numpy as jnp
from jax import lax
import numpy as np

D_MODEL = 2048
BATCH = 2
SEQ = 4096
DEPTH = 4

CTX_LEN = 256
GRID_W = 64
HY_WIDTH = 1024
ATT_HEADS = 8
KV_HEADS = 2
HEAD_DIM = 128
Q_PER_KV = ATT_HEADS // KV_HEADS
ATT_WIDTH = ATT_HEADS * HEAD_DIM
KV_WIDTH = KV_HEADS * HEAD_DIM
MIX_WIDTH = HY_WIDTH + ATT_WIDTH
IN_WIDTH = 3 * HY_WIDTH + ATT_WIDTH + 2 * KV_WIDTH
SPLITS = [3 * HY_WIDTH, 3 * HY_WIDTH + ATT_WIDTH, 3 * HY_WIDTH + ATT_WIDTH + KV_WIDTH]
HY_ORDER = 2
SHORT_CONV = 3
FILT_BANDS = 16
FILT_EMB = 1 + 2 * FILT_BANDS
FILT_HIDDEN = 64
DECAY_TARGET = 1e-2
FAST_DECAY_PCT = 0.3
SLOW_DECAY_PCT = 1.5
DECAY_MIN = math.log(DECAY_TARGET) / SLOW_DECAY_PCT
DECAY_MAX = math.log(DECAY_TARGET) / FAST_DECAY_PCT
WINDOW = 128
BLOCK = 128
ROPE_BASE = 10000.0
ROPE_PAIRS = HEAD_DIM // 4
SCALE = HEAD_DIM ** -0.5
NEG = -1e30
D_FF = -(-8 * D_MODEL // (3 * 256)) * 256
EPS = 1e-6

kernel_name = 'hybrid_hyena_swa_dit'


def rmsnorm(x, g):
    xf = x.astype(jnp.float32)
    y = xf * lax.rsqrt(jnp.mean(xf * xf, axis=-1, keepdims=True) + EPS)
    return y.astype(x.dtype) * g


def adaln(x, g, shift, scale):
    return rmsnorm(x, g) * (1.0 + scale) + shift


def short_conv(u, w, b):
    L = u.shape[1]
    pad = SHORT_CONV // 2
    up = jnp.pad(u, ((0, 0), (pad, pad), (0, 0)))
    return sum(up[:, j:j + L] * w[j] for j in range(SHORT_CONV)) + b


def hyena_filters(L, w1, b1, w2, b2, w3, freq):
    t = jnp.linspace(0.0, 1.0, L, dtype=jnp.float32)[:, None]
    w = (2.0 * math.pi / L) * jnp.arange(L, dtype=jnp.float32)[:, None]
    bands = jnp.linspace(1e-4, FILT_BANDS - 1, FILT_BANDS, dtype=jnp.float32)[None, :]
    emb = jnp.concatenate([t, jnp.cos(bands * w), -jnp.sin(bands * w)], axis=-1)
    hid = jnp.sin(freq * (emb @ w1 + b1))
    hid = jnp.sin(freq * (hid @ w2 + b2))
    h = (hid @ w3).reshape(L, HY_ORDER, 2, HY_WIDTH)
    deltas = jnp.abs(jnp.linspace(DECAY_MIN, DECAY_MAX, HY_WIDTH, dtype=jnp.float32))
    window = jnp.exp(-t * deltas)
    return h * window[:, None, None, :].astype(h.dtype)


def bidir_long_conv(z, h_fwd, h_bwd, bias):
    L = z.shape[1]
    taps = jnp.concatenate([h_fwd, jnp.zeros_like(h_fwd[:1]), h_bwd[:0:-1]], axis=0).astype(jnp.float32)
    tf = jnp.fft.rfft(taps, n=2 * L, axis=0)
    zf = jnp.fft.rfft(z.astype(jnp.float32), n=2 * L, axis=1)
    y = jnp.fft.irfft(zf * tf[None], n=2 * L, axis=1)[:, :L]
    return y.astype(z.dtype) + z * bias


def hyena_mixer(u, conv_w, conv_b, filt, filt_bias):
    u = short_conv(u, conv_w, conv_b)
    x1, x2, z = jnp.split(u, 3, axis=-1)
    for o, gate in enumerate((x1, x2)):
        z = gate * bidir_long_conv(z, filt[:, o, 0], filt[:, o, 1], filt_bias[o])
    return z


def rope_2d(x):
    L = x.shape[1]
    rows = L // GRID_W
    row = jnp.repeat(jnp.arange(rows, dtype=jnp.float32), GRID_W)
    col = jnp.tile(jnp.arange(GRID_W, dtype=jnp.float32), rows)
    inv = ROPE_BASE ** (-jnp.arange(ROPE_PAIRS, dtype=jnp.float32) / ROPE_PAIRS)

    def rot(xa, pos):
        ang = pos[:, None] * inv[None, :]
        cos = jnp.cos(ang)[None, :, None, :].astype(xa.dtype)
        sin = jnp.sin(ang)[None, :, None, :].astype(xa.dtype)
        a, b = jnp.split(xa, 2, axis=-1)
        return jnp.concatenate([a * cos - b * sin, a * sin + b * cos], axis=-1)

    half = HEAD_DIM // 2
    return jnp.concatenate([rot(x[..., :half], row), rot(x[..., half:], col)], axis=-1)


def window_attention(q, k, v, kc, vc, sink):
    B, L = q.shape[:2]
    C = kc.shape[1]
    nb = L // BLOCK
    nw = 3 * BLOCK
    qb = q.reshape(B, nb, BLOCK, KV_HEADS, Q_PER_KV, HEAD_DIM)

    def band(t):
        tp = jnp.pad(t, ((0, 0), (BLOCK, BLOCK), (0, 0), (0, 0))).reshape(B, nb + 2, BLOCK, KV_HEADS, HEAD_DIM)
        return jnp.concatenate([tp[:, :-2], tp[:, 1:-1], tp[:, 2:]], axis=2)

    kw, vw = band(k), band(v)
    s_win = jnp.einsum('bnqkgd,bnskd->bnkgqs', qb, kw).astype(jnp.float32) * SCALE
    qpos = jnp.arange(nb)[:, None, None] * BLOCK + jnp.arange(BLOCK)[None, :, None]
    kpos = jnp.arange(nb)[:, None, None] * BLOCK + jnp.arange(nw)[None, None, :] - BLOCK
    valid = (jnp.abs(kpos - qpos) <= WINDOW) & (kpos >= 0) & (kpos < L)
    s_win = jnp.where(valid[None, :, None, None], s_win, NEG)
    s_ctx = jnp.einsum('bnqkgd,bckd->bnkgqc', qb, kc).astype(jnp.float32) * SCALE
    s_sink = jnp.broadcast_to(sink.astype(jnp.float32).reshape(1, 1, KV_HEADS, Q_PER_KV, 1, 1),
                              s_win.shape[:-1] + (1,))
    p = jax.nn.softmax(jnp.concatenate([s_win, s_ctx, s_sink], axis=-1), axis=-1).astype(v.dtype)
    o = (jnp.einsum('bnkgqs,bnskd->bnqkgd', p[..., :nw], vw)
         + jnp.einsum('bnkgqc,bckd->bnqkgd', p[..., nw:nw + C], vc))
    return o.reshape(B, L, ATT_WIDTH)


def context_attention(qc, kc, vc, sink):
    B, C = qc.shape[:2]
    qg = qc.reshape(B, C, KV_HEADS, Q_PER_KV, HEAD_DIM)
    s = jnp.einsum('bqkgd,bckd->bkgqc', qg, kc).astype(jnp.float32) * SCALE
    s_sink = jnp.broadcast_to(sink.astype(jnp.float32).reshape(1, KV_HEADS, Q_PER_KV, 1, 1), s.shape[:-1] + (1,))
    p = jax.nn.softmax(jnp.concatenate([s, s_sink], axis=-1), axis=-1).astype(vc.dtype)
    o = jnp.einsum('bkgqc,bckd->bqkgd', p[..., :C], vc)
    return o.reshape(B, C, ATT_WIDTH)


def merge_groups(y_hy, y_at, g_hy, g_at, w_out):
    return jnp.concatenate([rmsnorm(y_hy, g_hy), rmsnorm(y_at, g_at)], axis=-1) @ w_out


def swiglu(h, w_gate, w_up, w_down):
    return (jax.nn.silu(h @ w_gate) * (h @ w_up)) @ w_down


def setup_inputs(seed: int = 0) -> dict:
    key = jax.random.key(seed)
    ks = jax.random.split(key, 32)

    def nrm(k, shape, s):
        return jax.random.normal(k, shape, jnp.float32) * s

    return {
        'x': nrm(ks[0], (BATCH, SEQ, D_MODEL), 1.0),
        'c': nrm(ks[1], (BATCH, D_MODEL), 1.0),
        'ctx': nrm(ks[2], (BATCH, CTX_LEN, D_MODEL), 1.0),
        'c_ctx': nrm(ks[3], (D_MODEL,), 1.0),
        'norm_mix_g': 1.0 + nrm(ks[4], (DEPTH, D_MODEL), 0.02),
        'norm_ffn_g': 1.0 + nrm(ks[5], (DEPTH, D_MODEL), 0.02),
        'w_mod': nrm(ks[6], (DEPTH, D_MODEL, 6 * D_MODEL), D_MODEL ** -0.5),
        'b_mod': nrm(ks[7], (DEPTH, 6 * D_MODEL), 0.02),
        'w_in': nrm(ks[8], (DEPTH, D_MODEL, IN_WIDTH), D_MODEL ** -0.5),
        'conv_w': nrm(ks[9], (DEPTH, SHORT_CONV, 3 * HY_WIDTH), SHORT_CONV ** -0.5),
        'conv_b': nrm(ks[10], (DEPTH, 3 * HY_WIDTH), 0.02),
        'filt_w1': nrm(ks[11], (DEPTH, FILT_EMB, FILT_HIDDEN), FILT_EMB ** -0.5),
        'filt_b1': nrm(ks[12], (DEPTH, FILT_HIDDEN), 0.1),
        'filt_w2': nrm(ks[13], (DEPTH, FILT_HIDDEN, FILT_HIDDEN), FILT_HIDDEN ** -0.5),
        'filt_b2': nrm(ks[14], (DEPTH, FILT_HIDDEN), 0.1),
        'filt_w3': nrm(ks[15], (DEPTH, FILT_HIDDEN, HY_ORDER * 2 * HY_WIDTH), 0.02),
        'filt_freq': 1.0 + nrm(ks[16], (DEPTH, FILT_HIDDEN), 0.02),
        'filt_bias': nrm(ks[17], (DEPTH, HY_ORDER, HY_WIDTH), 1.0),
        'attn_sink': nrm(ks[18], (DEPTH, ATT_HEADS), 0.5),
        'out_norm_hy': 1.0 + nrm(ks[19], (DEPTH, HY_WIDTH), 0.02),
        'out_norm_att': 1.0 + nrm(ks[20], (DEPTH, ATT_WIDTH), 0.02),
        'w_out': nrm(ks[21], (DEPTH, MIX_WIDTH, D_MODEL), MIX_WIDTH ** -0.5),
        'w_gate': nrm(ks[22], (DEPTH, D_MODEL, D_FF), D_MODEL ** -0.5),
        'w_up': nrm(ks[23], (DEPTH, D_MODEL, D_FF), D_MODEL ** -0.5),
        'w_down': nrm(ks[24], (DEPTH, D_FF, D_MODEL), D_FF ** -0.5),
        'final_g': 1.0 + nrm(ks[25], (D_MODEL,), 0.02),
    }


def reference(x, c, ctx, c_ctx, norm_mix_g, norm_ffn_g, w_mod, b_mod, w_in, conv_w, conv_b,
              filt_w1, filt_b1, filt_w2, filt_b2, filt_w3, filt_freq, filt_bias, attn_sink,
              out_norm_hy, out_norm_att, w_out, w_gate, w_up, w_down, final_g):
    B, L, _ = x.shape
    C = ctx.shape[1]
    silu_c = jax.nn.silu(c)
    silu_cc = jax.nn.silu(c_ctx)
    xc = ctx
    for i in range(DEPTH):
        last = i == DEPTH - 1
        mod = (silu_c @ w_mod[i] + b_mod[i]).reshape(B, 6, 1, D_MODEL)
        modc = (silu_cc @ w_mod[i] + b_mod[i]).reshape(6, D_MODEL)
        filt_args = (filt_w1[i], filt_b1[i], filt_w2[i], filt_b2[i], filt_w3[i], filt_freq[i])

        h = adaln(x, norm_mix_g[i], mod[:, 0], mod[:, 1])
        p_hy, q, k, v = jnp.split(h @ w_in[i], SPLITS, axis=-1)
        hc = adaln(xc, norm_mix_g[i], modc[0], modc[1])
        if last:
            kc, vc = jnp.split(hc @ w_in[i][:, SPLITS[1]:], 2, axis=-1)
        else:
            pc_hy, qc, kc, vc = jnp.split(hc @ w_in[i], SPLITS, axis=-1)
        kc = kc.reshape(B, C, KV_HEADS, HEAD_DIM)
        vc = vc.reshape(B, C, KV_HEADS, HEAD_DIM)

        q = rope_2d(q.reshape(B, L, ATT_HEADS, HEAD_DIM))
        k = rope_2d(k.reshape(B, L, KV_HEADS, HEAD_DIM))
        v = v.reshape(B, L, KV_HEADS, HEAD_DIM)
        y_at = window_attention(q, k, v, kc, vc, attn_sink[i])
        filt = hyena_filters(L, *filt_args)
        y_hy = hyena_mixer(p_hy, conv_w[i], conv_b[i], filt, filt_bias[i])
        x = x + mod[:, 2] * merge_groups(y_hy, y_at, out_norm_hy[i], out_norm_att[i], w_out[i])

        x = x + mod[:, 5] * swiglu(adaln(x, norm_ffn_g[i], mod[:, 3], mod[:, 4]), w_gate[i], w_up[i], w_down[i])

        if not last:
            filt_c = hyena_filters(C, *filt_args)
            yc_hy = hyena_mixer(pc_hy, conv_w[i], conv_b[i], filt_c, filt_bias[i])
            yc_at = context_attention(qc.reshape(B, C, ATT_HEADS, HEAD_DIM), kc, vc, attn_sink[i])
            xc = xc + modc[2] * merge_groups(yc_hy, yc_at, out_norm_hy[i], out_norm_att[i], w_out[i])
            xc = xc + modc[5] * swiglu(adaln(xc, norm_ffn_g[i], modc[3], modc[4]), w_gate[i], w_up[i], w_down[i])
    return rmsnorm(x, final_g)
```

```python
import math
import numpy as np
import concourse.bass as bass
import concourse.mybir as mybir
from concourse.bass_utils import run_bass_kernel_spmd
from contextlib import ExitStack

F32 = mybir.dt.float32
BF16 = mybir.dt.bfloat16
ALU = mybir.AluOpType
AF = mybir.ActivationFunctionType
AX = mybir.AxisListType

NCORES = 8
D = 2048
KT = D // 128
DEPTH = 4
L = 4096
CTX = 256
HY = 1024
IN_W = 4608
DFF = 5632
FT = DFF // 128
EPS = 1e-6
TM = 1024
TC = 64
T = TM + TC
CHUNKS = [(0, 512), (512, 512), (1024, 64)]
SCALE = 128 ** -0.5

ENG_ATTR = {'pe': 'tensor', 'dve': 'vector', 'act': 'scalar', 'pool': 'gpsimd', 'sp': 'sync'}
DMA_K = 8
SAME_ENGINE_SYNC = True


class Prog:
    def __init__(self):
        self.nc = bass.Bass("TRN2", target_bir_lowering=False)
        self.es = ExitStack()
        self.ops = {e: [] for e in ENG_ATTR}
        self.res = {}
        self.ndma = {e: 0 for e in ENG_ATTR}
        self.n_sb = 0
        self.out_tokens = []

    def dram(self, name, shape, dt, kind):
        return self.nc.dram_tensor(name, list(shape), dt, kind=kind).ap()

    def sbuf(self, shape, dt, name=None):
        self.n_sb += 1
        return self.es.enter_context(self.nc.sbuf_tensor("s_" + (name or f"sb{self.n_sb}"), list(shape), dt))

    def psum(self, shape, dt=F32, name=None):
        self.n_sb += 1
        return self.es.enter_context(self.nc.psum_tensor("p_" + (name or f"ps{self.n_sb}"), list(shape), dt))

    def _deps(self, eng, reads, writes, is_dma):
        deps = set()
        for k in reads:
            r = self.res.get(k)
            if r and r[0] is not None:
                deps.add(r[0])
        for k in writes:
            r = self.res.get(k)
            if r:
                if r[0] is not None:
                    deps.add(r[0])
                deps.update(r[1].values())
                deps.update(r[2])
        out = []
        for d in deps:
            if d[0] == 'c' and d[1] == eng and not is_dma:
                if eng == 'pe' or not SAME_ENGINE_SYNC:
                    continue
            out.append(d)
        return out

    def _update(self, tok, reads, writes):
        for k in reads:
            r = self.res.setdefault(k, [None, {}, []])
            if tok[0] == 'c':
                r[1][tok[1]] = tok
            else:
                r[2].append(tok)
        for k in writes:
            self.res[k] = [tok, {}, []]

    def op(self, eng, fn, reads=(), writes=()):
        deps = self._deps(eng, reads, writes, False)
        idx = len(self.ops[eng])
        tok = ('c', eng, idx)
        self.ops[eng].append(dict(fn=fn, deps=deps, dma=None))
        self._update(tok, reads, writes)
        return tok

    def dma(self, q, out, in_, reads=(), writes=(), is_output=False, **kw):
        deps = self._deps(q, reads, writes, True)
        n = self.ndma[q]
        self.ndma[q] += 1
        tok = ('d', q, n)
        if n >= DMA_K:
            deps.append(('d', q, n - DMA_K))
        self.ops[q].append(dict(fn=lambda e: e.dma_start(out=out, in_=in_, **kw), deps=deps, dma=n))
        self._update(tok, reads, writes)
        if is_output:
            self.out_tokens.append(tok)
        return tok

    def finalize(self):
        nc = self.nc
        final_eng = 'sp'
        self.ops[final_eng].append(dict(fn=None, deps=list(self.out_tokens), dma=None))
        signaled = {e: set() for e in ENG_ATTR}
        for e in ENG_ATTR:
            for o in self.ops[e]:
                for d in o['deps']:
                    if d[0] == 'c':
                        signaled[d[1]].add(d[2])
        signum = {}
        for e in ENG_ATTR:
            c = 0
            for i, o in enumerate(self.ops[e]):
                if o['dma'] is None and i in signaled[e]:
                    c += 1
                    signum[(e, i)] = c
        sems = {e: self.es.enter_context(nc.semaphore(f"s_{e}")) for e in ENG_ATTR}
        dsems = {}
        for q in ENG_ATTR:
            if self.ndma[q]:
                dsems[q] = [self.es.enter_context(nc.semaphore(f"d_{q}{j}")) for j in range(min(DMA_K, self.ndma[q]))]
        with nc.Block() as block:
            def make(e):
                def body(eng):
                    seen = {}
                    for i, o in enumerate(self.ops[e]):
                        waits = {}
                        for d in o['deps']:
                            if d[0] == 'c':
                                s = sems[d[1]]
                                v = signum[(d[1], d[2])]
                                key = ('c', d[1])
                            else:
                                s = dsems[d[1]][d[2] % DMA_K]
                                v = 16 * (d[2] // DMA_K + 1)
                                key = ('d', d[1], d[2] % DMA_K)
                            if seen.get(key, 0) >= v:
                                continue
                            if key not in waits or waits[key][1] < v:
                                waits[key] = (s, v)
                        for key, (s, v) in waits.items():
                            eng.wait_ge(s, v)
                            seen[key] = v
                        if o['fn'] is None:
                            continue
                        inst = o['fn'](eng)
                        if o['dma'] is not None:
                            inst.then_inc(dsems[e][o['dma'] % DMA_K], 16)
                        elif (e, i) in signum:
                            inst.then_inc(sems[e], 1)
                return body
            for e in ENG_ATTR:
                if self.ops[e]:
                    getattr(block, ENG_ATTR[e])(make(e))
        self.es.close()
        return nc


def mm(P, out, lhsT, rhs, start, stop, reads, writes):
    P.op('pe', lambda e: e.matmul(out, lhsT=lhsT, rhs=rhs, start=start, stop=stop), reads=reads, writes=writes)


def copy(P, eng, out, in_, reads, writes):
    if eng == 'act':
        P.op('act', lambda e: e.activation(out=out, in_=in_, func=AF.Identity), reads=reads, writes=writes)
    else:
        P.op(eng, lambda e: e.tensor_copy(out=out, in_=in_), reads=reads, writes=writes)


def tt(P, eng, out, in0, in1, op, reads, writes):
    P.op(eng, lambda e: e.tensor_tensor(out=out, in0=in0, in1=in1, op=op), reads=reads, writes=writes)


def ts(P, eng, out, in0, s1, s2, op0, op1, reads, writes):
    if op1 is None:
        P.op(eng, lambda e: e.tensor_scalar(out=out, in0=in0, scalar1=s1, scalar2=None, op0=op0), reads=reads, writes=writes)
    else:
        P.op(eng, lambda e: e.tensor_scalar(out=out, in0=in0, scalar1=s1, scalar2=s2, op0=op0, op1=op1), reads=reads, writes=writes)


def stt(P, out, in0, scalar, in1, op0, op1, reads, writes):
    P.op('dve', lambda e: e.scalar_tensor_tensor(out=out, in0=in0, scalar=scalar, in1=in1, op0=op0, op1=op1), reads=reads, writes=writes)


def act(P, out, in_, func, reads, writes, bias=None, scale=None):
    kw = {}
    if bias is not None:
        kw['bias'] = bias
    if scale is not None:
        kw['scale'] = scale
    P.op('act', lambda e: e.activation(out=out, in_=in_, func=func, **kw), reads=reads, writes=writes)


class Ctx:
    def __init__(self, P):
        self.P = P
        self.banks = [P.psum([128, 512], F32, name=f"bank{i}") for i in range(8)]
        self.ones = P.sbuf([128, 128], F32, name="ones")
        P.op('pool', lambda e: e.memset(self.ones[:], 1.0), writes=['ones'])
        self.eps = P.sbuf([128, 1], F32, name="epsc")
        P.op('pool', lambda e: e.memset(self.eps[:], EPS), writes=['epsc'])
        self.rr = 0

    def evac_eng(self):
        self.rr += 1
        return 'act' if self.rr % 2 else 'dve'


def emit_adaln(P, C, xT, hT, modt, gT, shift_row, scale_row, tag, nfeat=D):
    a = {}
    for cls in ('m', 'c'):
        a[cls] = P.sbuf([128, KT], F32, name=f"a_{tag}_{cls}")
        ts(P, 'dve', a[cls][:], modt[cls][:, scale_row * KT:(scale_row + 1) * KT], 1.0, None, ALU.add, None,
           reads=[('mod', cls)], writes=[('a', tag, cls)])
        tt(P, 'dve', a[cls][:], a[cls][:], gT[:], ALU.mult, reads=[('a', tag, cls), ('g', tag)], writes=[('a', tag, cls)])
    emit_rstd_apply(P, C, xT, hT, KT, nfeat, tag,
                    lambda ci, kt: (a['m' if ci < 2 else 'c'][:, kt:kt + 1], modt['m' if ci < 2 else 'c'][:, shift_row * KT + kt:shift_row * KT + kt + 1]),
                    lambda ci: [('a', tag, 'm' if ci < 2 else 'c'), ('mod', 'm' if ci < 2 else 'c')])


def get_tmp(P, C, name, shape, dt):
    if not hasattr(C, 'tmps'):
        C.tmps = {}
    if name not in C.tmps:
        C.tmps[name] = P.sbuf(shape, dt, name=f"T_{name}")
    return C.tmps[name]


def emit_rstd(P, C, src, nk, nfeat, ci, t0, n, rstd, src_key):
    sqb = [get_tmp(P, C, f"sq{i}", [128, 512], F32) for i in range(2)]
    stat = C.banks[7]
    for kt in range(nk):
        act(P, sqb[kt % 2][:, :n], src(kt), AF.Square, reads=[src_key], writes=[('sq', kt % 2)])
        mm(P, stat[:, :n], C.ones[:], sqb[kt % 2][:, :n], kt == 0, kt == nk - 1, reads=[('sq', kt % 2), 'ones'], writes=[('bank', 7)])
    act(P, rstd[:, :n], stat[:, :n], AF.Sqrt, reads=[('bank', 7), 'epsc'], writes=['rstd'], bias=C.eps[:, 0:1], scale=1.0 / nfeat)
    P.op('dve', (lambda o: (lambda e: e.reciprocal(out=o, in_=o)))(rstd[:, :n]), reads=['rstd'], writes=['rstd'])


def emit_rstd_apply(P, C, xT, hT, nk, nfeat, tag, coef, coef_keys, kt_off=0):
    hnb = [get_tmp(P, C, f"hn{i}", [128, 512], F32) for i in range(2)]
    rstd = get_tmp(P, C, "rstd", [128, 512], F32)
    for ci, (t0, n) in enumerate(CHUNKS):
        emit_rstd(P, C, lambda kt: xT[:, kt, t0:t0 + n], nk, nfeat, ci, t0, n, rstd, ('x', ci))
        for kt in range(nk):
            hn = hnb[kt % 2]
            sc, bi = coef(ci, kt)
            tt(P, 'dve', hn[:, :n], xT[:, kt, t0:t0 + n], rstd[:, :n], ALU.mult, reads=[('x', ci), 'rstd'], writes=[('hn', kt % 2)])
            act(P, hT[:, kt_off + kt, t0:t0 + n], hn[:, :n], AF.Identity, reads=[('hn', kt % 2)] + coef_keys(ci), writes=[('h', ci)], bias=bi, scale=sc)


def build_p1():
    P = Prog()
    C = Ctx(P)
    xTd = P.dram("xT", [D, T], F32, "ExternalInput")
    modmd = P.dram("modm", [128, 96], F32, "ExternalInput")
    modcd = P.dram("modc", [128, 96], F32, "ExternalInput")
    gd = P.dram("gmix", [128, KT], F32, "ExternalInput")
    wd = P.dram("w_in", [IN_W // 256, 128, KT, 256], F32, "ExternalInput")
    cosd = P.dram("cosT", [128, T], F32, "ExternalInput")
    sind = P.dram("sinT", [128, T], F32, "ExternalInput")
    permd = P.dram("perm", [128, 128], F32, "ExternalInput")
    uTd = P.dram("uT", [3 * HY, T], F32, "ExternalOutput")
    qkvd = P.dram("qkvT", [1536, T], BF16, "ExternalOutput")

    xT = P.sbuf([128, KT, T], F32, name="xT")
    hT = P.sbuf([128, KT, T], BF16, name="hT")
    modt = {'m': P.sbuf([128, 96], F32, name="modm"), 'c': P.sbuf([128, 96], F32, name="modc")}
    gT = P.sbuf([128, KT], F32, name="gT")
    cosT = P.sbuf([128, T], F32, name="cosT")
    sinT = P.sbuf([128, T], F32, name="sinT")
    perm = P.sbuf([128, 128], F32, name="perm")
    xr = xTd.rearrange("(kt p) t -> p kt t", p=128)
    for ci, (t0, n) in enumerate(CHUNKS):
        P.dma('sp', xT[:, :, t0:t0 + n], xr[:, :, t0:t0 + n], writes=[('x', ci)])
    P.dma('sp', modt['m'][:], modmd, writes=[('mod', 'm')])
    P.dma('sp', modt['c'][:], modcd, writes=[('mod', 'c')])
    P.dma('sp', gT[:], gd, writes=[('g', 'mix')])
    P.dma('sp', cosT[:], cosd, writes=['cos'])
    P.dma('sp', sinT[:], sind, writes=['sin'])
    P.dma('sp', perm[:], permd, writes=['perm'])

    emit_adaln(P, C, xT, hT, modt, gT, 0, 1, 'mix')

    wb = [P.sbuf([128, KT, 256], BF16, name=f"wb{i}") for i in range(2)]
    ust = [P.sbuf([128, T], F32, name=f"ust{i}") for i in range(2)]
    qst = [P.sbuf([128, T], BF16, name=f"qst{i}") for i in range(2)]
    qf = [P.sbuf([128, 512], F32, name=f"qf{i}") for i in range(2)]
    t1 = P.sbuf([128, 512], F32, name="t1")
    t2 = P.sbuf([128, 512], F32, name="t2")
    NCT = IN_W // 128
    nrope = 0
    for ct in range(NCT):
        blk, sub = ct // 2, ct % 2
        if sub == 0:
            P.dma('pool', wb[blk % 2][:], wd[blk], writes=[('wb', blk % 2)])
        w = wb[blk % 2]
        bset = (ct % 2) * 3
        for kt in range(KT):
            for ci, (t0, n) in enumerate(CHUNKS):
                mm(P, C.banks[bset + ci][:, :n], w[:, kt, sub * 128:(sub + 1) * 128], hT[:, kt, t0:t0 + n], kt == 0, kt == KT - 1,
                   reads=[('wb', blk % 2), ('h', ci)], writes=[('bank', bset + ci)])
        if ct < 24:
            st = ust[ct % 2]
            for ci, (t0, n) in enumerate(CHUNKS):
                copy(P, C.evac_eng(), st[:, t0:t0 + n], C.banks[bset + ci][:, :n], reads=[('bank', bset + ci)], writes=[('ust', ct % 2)])
            P.dma('sp', uTd[ct * 128:(ct + 1) * 128, :], st[:], reads=[('ust', ct % 2)], is_output=True)
        elif ct < 34:
            st = qst[ct % 2]
            for ci, (t0, n) in enumerate(CHUNKS):
                q = qf[nrope % 2]
                pb = C.banks[6]
                nrope += 1
                copy(P, 'act', q[:, :n], C.banks[bset + ci][:, :n], reads=[('bank', bset + ci)], writes=[('qf', nrope % 2)])
                mm(P, pb[:, :n], perm[:], q[:, :n], True, True, reads=[('qf', nrope % 2), 'perm'], writes=[('bank', 6)])
                tt(P, 'dve', t1[:, :n], q[:, :n], cosT[:, t0:t0 + n], ALU.mult, reads=[('qf', nrope % 2), 'cos'], writes=['t1'])
                tt(P, 'dve', t2[:, :n], pb[:, :n], sinT[:, t0:t0 + n], ALU.mult, reads=[('bank', 6), 'sin'], writes=['t2'])
                tt(P, 'dve', st[:, t0:t0 + n], t1[:, :n], t2[:, :n], ALU.add, reads=['t1', 't2'], writes=[('qst', ct % 2)])
            P.dma('sp', qkvd[(ct - 24) * 128:(ct - 23) * 128, :], st[:], reads=[('qst', ct % 2)], is_output=True)
        else:
            st = qst[ct % 2]
            for ci, (t0, n) in enumerate(CHUNKS):
                copy(P, C.evac_eng(), st[:, t0:t0 + n], C.banks[bset + ci][:, :n], reads=[('bank', bset + ci)], writes=[('qst', ct % 2)])
            P.dma('sp', qkvd[(ct - 24) * 128:(ct - 23) * 128, :], st[:], reads=[('qst', ct % 2)], is_output=True)
    return P.finalize()


def core_tokens(r):
    b, j = r // 4, r % 4
    return b, j


def rope_tables():
    n = np.arange(L)
    row = (n // 64).astype(np.float32)
    col = (n % 64).astype(np.float32)
    inv = (10000.0 ** (-np.arange(32, dtype=np.float32) / 32)).astype(np.float32)
    cos = np.zeros((128, L), np.float32)
    sin = np.zeros((128, L), np.float32)
    for base, pos in ((0, row), (64, col)):
        ang = (pos[None, :] * inv[:, None]).astype(np.float32)
        cos[base:base + 32] = np.cos(ang)
        cos[base + 32:base + 64] = np.cos(ang)
        sin[base:base + 32] = -np.sin(ang)
        sin[base + 32:base + 64] = np.sin(ang)
    return cos, sin


def perm_matrix():
    pm = np.zeros((128, 128), np.float32)
    for m in range(128):
        k = m + 32 if (m % 64) < 32 else m - 32
        pm[k, m] = 1.0
    return pm


def vecT(v):
    v = np.asarray(v, np.float32).reshape(-1, KT, 128)
    return np.ascontiguousarray(v.transpose(2, 0, 1).reshape(128, -1))


def w_blocks(w, cols):
    K, N = w.shape
    return np.ascontiguousarray(w.reshape(K // 128, 128, N // cols, cols).transpose(2, 1, 0, 3))


PI = math.pi


def emit_wrap(P, t, m, key_t, key_m):
    for _ in range(2):
        ts(P, 'dve', m, t, PI, -2 * PI, ALU.is_gt, ALU.mult, reads=[key_t], writes=[key_m])
        tt(P, 'dve', t, t, m, ALU.add, reads=[key_t, key_m], writes=[key_t])
        ts(P, 'dve', m, t, -PI, 2 * PI, ALU.is_lt, ALU.mult, reads=[key_t], writes=[key_m])
        tt(P, 'dve', t, t, m, ALU.add, reads=[key_t, key_m], writes=[key_t])


def emit_conv(P, out, tmp, ug, cw, cb, g, t0, n, reads, writes, tmpkey):
    ts(P, 'dve', tmp, ug[:, t0:t0 + n], cw[:, 3 * g:3 * g + 1], cb[:, g:g + 1], ALU.mult, ALU.add, reads=reads + ['cw'], writes=[tmpkey])
    stt(P, tmp, ug[:, t0 + 1:t0 + 1 + n], cw[:, 3 * g + 1:3 * g + 2], tmp, ALU.mult, ALU.add, reads=reads + [tmpkey, 'cw'], writes=[tmpkey])
    stt(P, out, ug[:, t0 + 2:t0 + 2 + n], cw[:, 3 * g + 2:3 * g + 3], tmp, ALU.mult, ALU.add, reads=reads + [tmpkey, 'cw'], writes=writes)


def emit_hyena(P, C, v, Lx, ud, embd, wind, tabd, wfd, outd, S):
    NT = Lx // 128
    NCH = Lx // 256
    fw = min(256, Lx)
    nfc = Lx // fw
    qn = fw // 128
    MT = P.sbuf([128, NT, 3, 256], BF16, name=f"MT{v}")
    K1 = P.sbuf([128, NT, 2, 128], BF16, name=f"K1{v}")
    YH = P.sbuf([128, NT, 2, 2, 128], BF16, name=f"YH{v}")
    tab = [P.sbuf([128, NT, 256], BF16, name=f"tab{v}{i}") for i in range(2)]
    ug = [P.sbuf([128, Lx + 2], F32, name=f"ug{v}{b}") for b in range(2)]
    wfs = P.sbuf([128, NT], F32, name=f"wf{v}")
    def tmp(name, shape, dt):
        if ('tmp', name) not in S:
            S[('tmp', name)] = P.sbuf(shape, dt, name=f"T_{name}")
        return S[('tmp', name)]
    embs = [tmp(f"emb{i}", [33, 256], F32) for i in range(2)]
    wins = [tmp(f"win{i}", [128, 256], F32) for i in range(2)]
    tA = tmp("tA", [128, 256], F32)
    tM = tmp("tM", [128, 256], F32)
    hid1 = tmp("hid1", [64, 256], F32)
    hid2 = tmp("hid2", [64, 256], F32)
    tf = tmp("tf", [128, 256], F32)
    tb = tmp("tb", [128, 256], F32)
    sdb = [tmp(f"sdb{i}", [128, 256], BF16) for i in range(2)]
    ctmp = tmp("ctmp", [128, 256], F32)
    zb = tmp("zb", [128, 256], BF16)
    kc = tmp("kc", [128, 128], F32)
    ks = tmp("ks", [128, 128], F32)
    pa = tmp("pa", [128, 2, 128], F32)
    pb_ = tmp("pb", [128, 2, 128], F32)
    xg = tmp("xg", [128, 256], F32)
    ost = [tmp(f"ost{i}", [128, 256], F32) for i in range(2)]
    P.dma('sp', wfs[:], wfd, writes=[('wf', v)])
    B = C.banks
    ntab = [0]

    def load_tab(idx):
        s = ntab[0] % 2
        ntab[0] += 1
        P.dma('sp', tab[s][:], tabd[idx], writes=[('tab', v, s)])
        return tab[s], ('tab', v, s)

    for j in range(nfc):
        e_, w_ = embs[j % 2], wins[j % 2]
        P.dma('sp', e_[:], embd[:, j * fw:(j + 1) * fw], writes=[('emb', j % 2)])
        P.dma('sp', w_[:], wind[:, j * fw:(j + 1) * fw], writes=[('win', j % 2)])
        mm(P, B[0][0:64, :fw], S['w1'][:], e_[:], True, True, reads=['fp', ('emb', j % 2)], writes=[('bank', 0)])
        ts(P, 'dve', tA[0:64, :], B[0][0:64, :fw], S['b1'][:, 0:1], S['fq'][:, 0:1], ALU.add, ALU.mult, reads=[('bank', 0), 'fp'], writes=[('tA',)])
        emit_wrap(P, tA[0:64, :], tM[0:64, :], ('tA',), ('tM',))
        act(P, hid1[:], tA[0:64, :], AF.Sin, reads=[('tA',)], writes=[('hid1',)])
        mm(P, B[1][0:64, :fw], S['w2'][:], hid1[:], True, True, reads=['fp', ('hid1',)], writes=[('bank', 1)])
        ts(P, 'dve', tA[0:64, :], B[1][0:64, :fw], S['b2'][:, 0:1], S['fq'][:, 0:1], ALU.add, ALU.mult, reads=[('bank', 1), 'fp'], writes=[('tA',)])
        emit_wrap(P, tA[0:64, :], tM[0:64, :], ('tA',), ('tM',))
        act(P, hid2[:], tA[0:64, :], AF.Sin, reads=[('tA',)], writes=[('hid2',)])
        for o in range(2):
            mm(P, B[2][:, :fw], S['w3'][:, 2 * o, :], hid2[:], True, True, reads=['fp', ('hid2',)], writes=[('bank', 2)])
            mm(P, B[3][:, :fw], S['w3'][:, 2 * o + 1, :], hid2[:], True, True, reads=['fp', ('hid2',)], writes=[('bank', 3)])
            tt(P, 'dve', tf[:], B[2][:, :fw], w_[:], ALU.mult, reads=[('bank', 2), ('win', j % 2)], writes=[('tf',)])
            tt(P, 'dve', tb[:], B[3][:, :fw], w_[:], ALU.mult, reads=[('bank', 3), ('win', j % 2)], writes=[('tb',)])
            if j == 0:
                P.op('dve', (lambda a_: lambda e: e.memset(a_, 0.0))(tb[:, 0:1]), reads=[('tb',)], writes=[('tb',)])
            tt(P, 'dve', sdb[0][:], tf[:], tb[:], ALU.add, reads=[('tf',), ('tb',)], writes=[('sdb', 0)])
            tt(P, 'dve', sdb[1][:], tf[:], tb[:], ALU.subtract, reads=[('tf',), ('tb',)], writes=[('sdb', 1)])
            for sd in range(2):
                pbk = B[4 + sd][:].bitcast(BF16)
                for q in range(qn):
                    P.op('pe', (lambda o_, i_: lambda e: e.transpose(out=o_, in_=i_, identity=S['idn'][:]))(pbk[:, q * 128:(q + 1) * 128], sdb[sd][:, q * 128:(q + 1) * 128]),
                         reads=[('sdb', sd), 'idn'], writes=[('bank', 4 + sd)])
                copy(P, 'act', MT[:, j * qn:(j + 1) * qn, 1 + sd, o * 128:(o + 1) * 128], pbk[:, 0:qn * 128].rearrange("p (a b) -> p a b", a=qn),
                     reads=[('bank', 4 + sd)], writes=[('MT', v, 1 + sd)])

    def load_u(g):
        for b in range(2):
            P.dma('sp', ug[b][:], ud[:, g, b, :], writes=[('ug', v, b)])

    load_u(2)
    for b in range(2):
        for j in range(nfc):
            emit_conv(P, zb[:, :fw], ctmp[:, :fw], ug[b], S['cw'], S['cb'], 2, j * fw, fw, [('ug', v, b)], [('zb',)], ('ctmp',))
            pbk = B[6 + (j % 2)][:].bitcast(BF16)
            for q in range(qn):
                P.op('pe', (lambda o_, i_: lambda e: e.transpose(out=o_, in_=i_, identity=S['idn'][:]))(pbk[:, q * 128:(q + 1) * 128], zb[:, q * 128:(q + 1) * 128]),
                     reads=[('zb',), 'idn'], writes=[('bank', 6 + (j % 2))])
            copy(P, 'act', MT[:, j * qn:(j + 1) * qn, 0, b * 128:(b + 1) * 128], pbk[:, 0:qn * 128].rearrange("p (a b) -> p a b", a=qn),
                 reads=[('bank', 6 + (j % 2))], writes=[('MT', v, 0)])

    for o in range(2):
        for c in range(NCH):
            bs = (c % 2) * 4
            tC, kC = load_tab(2 * c)
            tS, kS = load_tab(2 * c + 1)
            for (tb_, kk, boff, slots) in ((tC, kC, 0, (0, 1)), (tS, kS, 2, (0, 2))):
                W = 512 if o == 0 else 256
                for t_ in range(NT):
                    if o == 0:
                        rhs = MT[:, t_, 0:2, :] if slots == (0, 1) else MT[:, t_, 0::2, :]
                        rd = [('MT', v, slots[0]), ('MT', v, slots[1]), kk]
                    else:
                        rhs = MT[:, t_, 0, :]
                        rd = [('MT', v, 0), kk]
                    for fs in range(2):
                        mm(P, B[bs + boff + fs][:, :W], tb_[:, t_, fs * 128:(fs + 1) * 128], rhs, t_ == 0, t_ == NT - 1, reads=rd, writes=[('bank', bs + boff + fs)])
            for fs in range(2):
                ft = 2 * c + fs
                PC, PS = B[bs + fs], B[bs + 2 + fs]
                rdb = [('bank', bs + fs), ('bank', bs + 2 + fs)]
                wf1 = wfs[:, ft:ft + 1]
                if o == 0:
                    tt(P, 'dve', kc[:], PC[:, 256:384], S['fb'][:, 0, :], ALU.add, reads=rdb + ['fp'], writes=[('kc',)])
                    ts(P, 'dve', kc[:], kc[:], wf1, None, ALU.mult, None, reads=[('kc',), ('wf', v)], writes=[('kc',)])
                    ts(P, 'dve', ks[:], PS[:, 256:384], wf1, None, ALU.mult, None, reads=rdb + [('wf', v)], writes=[('ks',)])
                    tt(P, 'dve', pa[:, 0, :], PC[:, 384:512], S['fb'][:, 1, :], ALU.add, reads=rdb + ['fp'], writes=[('pa',)])
                    ts(P, 'dve', K1[:, ft, 0, :], pa[:, 0, :], wf1, None, ALU.mult, None, reads=[('pa',), ('wf', v)], writes=[('K1', v)])
                    ts(P, 'dve', K1[:, ft, 1, :], PS[:, 384:512], wf1, None, ALU.mult, None, reads=rdb + [('wf', v)], writes=[('K1', v)])
                    kcb = kc[:].unsqueeze(1).broadcast_to([128, 2, 128])
                    ksb = ks[:].unsqueeze(1).broadcast_to([128, 2, 128])
                    krd = [('kc',), ('ks',)]
                else:
                    kcb = K1[:, ft, 0, :].unsqueeze(1).broadcast_to([128, 2, 128])
                    ksb = K1[:, ft, 1, :].unsqueeze(1).broadcast_to([128, 2, 128])
                    krd = [('K1', v)]
                Zc = PC[:, 0:256].rearrange("p (a b) -> p a b", a=2)
                Zs = PS[:, 0:256].rearrange("p (a b) -> p a b", a=2)
                tt(P, 'dve', pa[:], Zc, kcb, ALU.mult, reads=rdb + krd, writes=[('pa',)])
                tt(P, 'dve', pb_[:], Zs, ksb, ALU.mult, reads=rdb + krd, writes=[('pb',)])
                tt(P, 'dve', YH[:, ft, 0, :, :], pa[:], pb_[:], ALU.subtract, reads=[('pa',), ('pb',)], writes=[('YH', v)])
                tt(P, 'dve', pa[:], Zc, ksb, ALU.mult, reads=rdb + krd, writes=[('pa',)])
                tt(P, 'dve', pb_[:], Zs, kcb, ALU.mult, reads=rdb + krd, writes=[('pb',)])
                tt(P, 'dve', YH[:, ft, 1, :, :], pa[:], pb_[:], ALU.add, reads=[('pa',), ('pb',)], writes=[('YH', v)])
        load_u(o)
        for c in range(NCH):
            tC, kC = load_tab(2 * c)
            tS, kS = load_tab(2 * c + 1)
            for b in range(2):
                bk = (c % 2) * 2 + b
                n_acc = 2 * NT
                i_acc = 0
                for (tb_, kk, cs) in ((tC, kC, 0), (tS, kS, 1)):
                    for ft in range(NT):
                        mm(P, B[bk][:, :256], YH[:, ft, cs, b, :], tb_[:, ft, :], i_acc == 0, i_acc == n_acc - 1, reads=[('YH', v), kk], writes=[('bank', bk)])
                        i_acc += 1
                emit_conv(P, xg[:], ctmp[:, :256], ug[b], S['cw'], S['cb'], o, c * 256, 256, [('ug', v, b)], [('xg',)], ('ctmp',))
                if o == 0:
                    tt(P, 'dve', zb[:, :256], xg[:], B[bk][:, :256], ALU.mult, reads=[('xg',), ('bank', bk)], writes=[('zb',)])
                    pbk = B[4 + (c % 2) * 2 + b][:].bitcast(BF16)
                    for q in range(2):
                        P.op('pe', (lambda o_, i_: lambda e: e.transpose(out=o_, in_=i_, identity=S['idn'][:]))(pbk[:, q * 128:(q + 1) * 128], zb[:, q * 128:(q + 1) * 128]),
                             reads=[('zb',), 'idn'], writes=[('bank', 4 + (c % 2) * 2 + b)])
                    copy(P, 'act', MT[:, 2 * c:2 * c + 2, 0, b * 128:(b + 1) * 128], pbk[:, 0:256].rearrange("p (a b) -> p a b", a=2),
                         reads=[('bank', 4 + (c % 2) * 2 + b)], writes=[('MT', v, 0)])
                else:
                    os_ = ost[b]
                    tt(P, 'dve', os_[:], xg[:], B[bk][:, :256], ALU.mult, reads=[('xg',), ('bank', bk)], writes=[('ost', b)])
                    P.dma('pool', outd[:, b, c * 256:(c + 1) * 256], os_[:], reads=[('ost', b)], is_output=True)


def build_h():
    P = Prog()
    C = Ctx(P)
    I = "ExternalInput"
    umd = P.dram("um", [128, 3, 2, L + 2], F32, I)
    ucd = P.dram("uc", [128, 3, 2, CTX + 2], F32, I)
    dd = dict(cw=P.dram("cw", [128, 9], F32, I), cb=P.dram("cb", [128, 3], F32, I),
              w1=P.dram("w1", [33, 64], F32, I), b1=P.dram("b1", [64, 1], F32, I),
              w2=P.dram("w2", [64, 64], F32, I), b2=P.dram("b2", [64, 1], F32, I),
              fq=P.dram("fq", [64, 1], F32, I), w3=P.dram("w3", [64, 4, 128], F32, I),
              fb=P.dram("fb", [128, 2, 128], F32, I), idn=P.dram("idn", [128, 128], BF16, I))
    embm = P.dram("embm", [33, L], F32, I)
    embc = P.dram("embc", [33, CTX], F32, I)
    winm = P.dram("winm", [128, L], F32, I)
    winc = P.dram("winc", [128, CTX], F32, I)
    tabm = P.dram("tabm", [2 * (L // 256), 128, L // 128, 256], BF16, I)
    tabc = P.dram("tabc", [2, 128, CTX // 128, 256], BF16, I)
    wfm = P.dram("wfm", [128, L // 128], F32, I)
    wfc = P.dram("wfc", [128, CTX // 128], F32, I)
    yhm = P.dram("yhm", [128, 2, L], F32, "ExternalOutput")
    yhc = P.dram("yhc", [128, 2, CTX], F32, "ExternalOutput")
    S = {}
    shapes = dict(cw=[128, 9], cb=[128, 3], w1=[33, 64], b1=[64, 1], w2=[64, 64], b2=[64, 1], fq=[64, 1], w3=[64, 4, 128], fb=[128, 2, 128])
    for k, shp in shapes.items():
        S[k] = P.sbuf(shp, F32, name=f"S_{k}")
        P.dma('sp', S[k][:], dd[k], writes=['fp' if k not in ('cw', 'cb') else 'cw'])
    S['idn'] = P.sbuf([128, 128], BF16, name="S_idn")
    P.dma('sp', S['idn'][:], dd['idn'], writes=['idn'])
    emit_hyena(P, C, 'c', CTX, ucd, embc, winc, tabc, wfc, yhc, S)
    emit_hyena(P, C, 'm', L, umd, embm, winm, tabm, wfm, yhm, S)
    return P.finalize()


def dft_tables(Lx):
    import ml_dtypes
    N = 2 * Lx - 1
    t = np.arange(Lx, dtype=np.int64)
    ph = (np.outer(t, t) % N).astype(np.float64) * (2.0 * np.pi / N)
    NT, NCH = Lx // 128, Lx // 256
    out = np.zeros((2 * NCH, 128, NT, 256), dtype=ml_dtypes.bfloat16)
    for nm, fn in ((0, np.cos), (1, np.sin)):
        M = fn(ph).astype(np.float32)
        M = M.reshape(NT, 128, NCH, 256).transpose(2, 1, 0, 3)
        out[nm::2] = M.astype(ml_dtypes.bfloat16)
    wf = np.full((Lx,), 2.0 / N, np.float32)
    wf[0] = 1.0 / N
    wfT = np.ascontiguousarray(wf.reshape(NT, 128).T)
    return out, wfT


def filt_consts(Lx):
    t = np.linspace(0.0, 1.0, Lx, dtype=np.float32)[:, None]
    w = (2.0 * np.pi / Lx) * np.arange(Lx, dtype=np.float32)[:, None]
    bands = np.linspace(1e-4, 16 - 1, 16, dtype=np.float32)[None, :]
    emb = np.concatenate([t, np.cos(bands * w), -np.sin(bands * w)], axis=-1).astype(np.float32)
    dmin = math.log(1e-2) / 1.5
    dmax = math.log(1e-2) / 0.3
    deltas = np.abs(np.linspace(dmin, dmax, HY, dtype=np.float32))
    window = np.exp(-t * deltas[None, :]).astype(np.float32)
    return np.ascontiguousarray(emb.T), np.ascontiguousarray(window.T)


NQT = 16


def build_a():
    P = Prog()
    C = Ctx(P)
    I = "ExternalInput"
    qd = P.dram("qT", [128, 4, NQT * 128], BF16, I)
    kd = P.dram("kT", [128, (NQT + 2) * 128], BF16, I)
    vd = P.dram("v", [128, NQT + 2, 128], BF16, I)
    kcd = P.dram("kcT", [128, CTX], BF16, I)
    vcd = P.dram("vc", [128, 2, 128], BF16, I)
    qcd = P.dram("qcT", [128, 4, 128], BF16, I)
    skd = P.dram("sink", [128, 4], F32, I)
    mkd = P.dram("masks", [128, 4, 128], BF16, I)
    od = P.dram("oT", [128, 4, (NQT + 1) * 128], F32, "ExternalOutput")
    qT = P.sbuf([128, 4, NQT * 128], BF16, name="qT")
    kT = P.sbuf([128, (NQT + 2) * 128], BF16, name="kT")
    vv = P.sbuf([128, NQT + 2, 128], BF16, name="vv")
    kcT = P.sbuf([128, CTX], BF16, name="kcT")
    vc = P.sbuf([128, 2, 128], BF16, name="vc")
    qcT = P.sbuf([128, 4, 128], BF16, name="qcT")
    sk = P.sbuf([128, 4], F32, name="sk")
    mk = P.sbuf([128, 4, 128], BF16, name="mk")
    onesb = P.sbuf([128, 128], BF16, name="onesb")
    P.op('pool', lambda e: e.memset(onesb[:], 1.0), writes=['onesb'])
    for (t_, d_, k_) in ((qT, qd, 'qT'), (kT, kd, 'kT'), (vv, vd, 'vv'), (kcT, kcd, 'kcT'), (vc, vcd, 'vc'), (qcT, qcd, 'qcT'), (sk, skd, 'sk'), (mk, mkd, 'mk')):
        P.dma('sp', t_[:], d_, writes=[k_])
    act(P, sk[:], sk[:], AF.Exp, reads=['sk'], writes=['sk'])
    E = [P.sbuf([128, 4, 128], BF16, name=f"E{i}") for i in range(3)]
    den = P.sbuf([128, 4, 128], F32, name="den")
    ost = [P.sbuf([128, 4, 128], F32, name=f"aost{i}") for i in range(2)]
    B = C.banks
    ne = 0
    for qt in range(NQT + 1):
        if qt < NQT:
            qsl = qT[:, :, qt * 128:(qt + 1) * 128]
            qk = 'qT'
            keys = [(kT[:, qt * 128:(qt + 1) * 128], vv[:, qt, :], 2 if qt == 0 else 0),
                    (kT[:, (qt + 1) * 128:(qt + 2) * 128], vv[:, qt + 1, :], None),
                    (kT[:, (qt + 2) * 128:(qt + 3) * 128], vv[:, qt + 2, :], 3 if qt == NQT - 1 else 1),
                    (kcT[:, 0:128], vc[:, 0, :], None), (kcT[:, 128:256], vc[:, 1, :], None)]
        else:
            qsl = qcT[:]
            qk = 'qcT'
            keys = [(kcT[:, 0:128], vc[:, 0, :], None), (kcT[:, 128:256], vc[:, 1, :], None)]
        bO, bD = 2 + (qt % 2) * 2, 3 + (qt % 2) * 2
        for ei, (ka, va, mi) in enumerate(keys):
            bS = ne % 2
            e_ = E[ne % 3]
            ek = ('E', ne % 3)
            ne += 1
            mm(P, B[bS][:, :512], ka, qsl, True, True, reads=['kT', 'kcT', qk], writes=[('bank', bS)])
            act(P, e_[:], B[bS][:, :512].rearrange("p (a b) -> p a b", a=4), AF.Exp, reads=[('bank', bS)], writes=[ek], scale=SCALE)
            if mi is not None:
                tt(P, 'dve', e_[:], e_[:], mk[:, mi, :].unsqueeze(1).broadcast_to([128, 4, 128]), ALU.mult, reads=[ek, 'mk'], writes=[ek])
            last = ei == len(keys) - 1
            mm(P, B[bO][:, :512], va, e_[:], ei == 0, last, reads=['vv', 'vc', ek], writes=[('bank', bO)])
            mm(P, B[bD][:, :512], onesb[:], e_[:], ei == 0, last, reads=['onesb', ek], writes=[('bank', bD)])
        tt(P, 'dve', den[:], B[bD][:, :512].rearrange("p (a b) -> p a b", a=4), sk[:].unsqueeze(2).broadcast_to([128, 4, 128]), ALU.add,
           reads=[('bank', bD), 'sk'], writes=['den'])
        P.op('dve', lambda e: e.reciprocal(out=den[:], in_=den[:]), reads=['den'], writes=['den'])
        o_ = ost[qt % 2]
        tt(P, 'dve', o_[:], B[bO][:, :512].rearrange("p (a b) -> p a b", a=4), den[:], ALU.mult, reads=[('bank', bO), 'den'], writes=[('aost', qt % 2)])
        P.dma('pool', od[:, :, qt * 128:(qt + 1) * 128], o_[:], reads=[('aost', qt % 2)], is_output=True)
    return P.finalize()


def attn_masks(half):
    import ml_dtypes
    j = np.arange(128)[:, None]
    i = np.arange(128)[None, :]
    mL = (j >= i).astype(np.float32)
    mR = (j <= i).astype(np.float32)
    z = np.zeros_like(mL)
    m = np.stack([mL, mR, z if half == 0 else mL, z if half == 1 else mR], axis=1)
    return np.ascontiguousarray(m).astype(ml_dtypes.bfloat16)


def host_a_inputs(i, attn_sink, qkvT_full, qkvcT_full):
    maps = []
    for r in range(NCORES):
        b, kv, half = r // 4, (r // 2) % 2, r % 2
        t0 = half * NQT * 128
        qT = qkvT_full[kv * 512:(kv + 1) * 512, b, t0:t0 + NQT * 128].reshape(4, 128, NQT * 128).transpose(1, 0, 2)
        kfull = np.pad(qkvT_full[1024 + kv * 128:1024 + (kv + 1) * 128, b, :], ((0, 0), (128, 128)))
        vfull = np.pad(qkvT_full[1280 + kv * 128:1280 + (kv + 1) * 128, b, :], ((0, 0), (128, 128)))
        kT = kfull[:, t0:t0 + (NQT + 2) * 128]
        v = vfull[:, t0:t0 + (NQT + 2) * 128].reshape(128, NQT + 2, 128).transpose(2, 1, 0)
        kcT = qkvcT_full[1024 + kv * 128:1024 + (kv + 1) * 128, b, :]
        vc = qkvcT_full[1280 + kv * 128:1280 + (kv + 1) * 128, b, :].reshape(128, 2, 128).transpose(2, 1, 0)
        qcT = qkvcT_full[kv * 512:(kv + 1) * 512, b, half * 128:(half + 1) * 128].reshape(4, 128, 128).transpose(1, 0, 2)
        sink = np.broadcast_to(attn_sink[i][kv * 4:(kv + 1) * 4][None, :], (128, 4)).astype(np.float32)
        maps.append(dict(qT=np.ascontiguousarray(qT), kT=np.ascontiguousarray(kT), v=np.ascontiguousarray(v), kcT=np.ascontiguousarray(kcT),
                         vc=np.ascontiguousarray(vc), qcT=np.ascontiguousarray(qcT), sink=np.ascontiguousarray(sink), masks=attn_masks(half)))
    return maps


def host_a_gather(results):
    y = np.zeros((1024, 2, L), np.float32)
    yc = np.zeros((1024, 2, CTX), np.float32)
    for r in range(NCORES):
        b, kv, half = r // 4, (r // 2) % 2, r % 2
        o = results[r]['oT']
        o = o.transpose(1, 0, 2).reshape(512, (NQT + 1) * 128)
        y[kv * 512:(kv + 1) * 512, b, half * NQT * 128:(half + 1) * NQT * 128] = o[:, :NQT * 128]
        yc[kv * 512:(kv + 1) * 512, b, half * 128:(half + 1) * 128] = o[:, NQT * 128:]
    return y, yc


NGRP = 4
GF = FT // NGRP


def build_p2(final=False):
    P = Prog()
    C = Ctx(P)
    I = "ExternalInput"
    xTd = P.dram("xT", [D, T], F32, I)
    yd = [P.dram("yhyT", [HY, T], F32, I), P.dram("yatT", [HY, T], F32, I)]
    modmd = P.dram("modm", [128, 96], F32, I)
    modcd = P.dram("modc", [128, 96], F32, I)
    ggd = [P.dram("ghy", [128, 8], F32, I), P.dram("gat", [128, 8], F32, I)]
    gfd = P.dram("gffn", [128, KT], F32, I)
    wod = P.dram("w_out", [KT, 128, KT, 128], F32, I)
    wgd = P.dram("w_gate", [FT, 128, KT, 128], F32, I)
    wud = P.dram("w_up", [FT, 128, KT, 128], F32, I)
    wdd = P.dram("w_down", [NGRP, D // 256, 128, GF, 256], F32, I)
    xod = P.dram("xTo", [D, T], F32, "ExternalOutput")
    if final:
        gfin_d = P.dram("gfin", [128, KT], F32, I)
        outd = P.dram("outT", [D, TM], F32, "ExternalOutput")

    xT = P.sbuf([128, KT, T], F32, name="xT")
    hT = P.sbuf([128, KT, T], BF16, name="hT")
    modt = {'m': P.sbuf([128, 96], F32, name="modm"), 'c': P.sbuf([128, 96], F32, name="modc")}
    gg = [P.sbuf([128, 8], F32, name="ghy"), P.sbuf([128, 8], F32, name="gat")]
    gf = P.sbuf([128, KT], F32, name="gffn")
    xr = xTd.rearrange("(kt p) t -> p kt t", p=128)
    for ci, (t0, n) in enumerate(CHUNKS):
        P.dma('sp', xT[:, :, t0:t0 + n], xr[:, :, t0:t0 + n], writes=[('x', ci)])
    P.dma('sp', modt['m'][:], modmd, writes=[('mod', 'm')])
    P.dma('sp', modt['c'][:], modcd, writes=[('mod', 'c')])
    P.dma('sp', gg[0][:], ggd[0], writes=[('gg', 0)])
    P.dma('sp', gg[1][:], ggd[1], writes=[('gg', 1)])
    P.dma('sp', gf[:], gfd, writes=[('g', 'ffn')])
    B = C.banks
    hkeys = [('h', ci) for ci in range(3)]

    yst = [P.sbuf([128, T], F32, name=f"yst{i}") for i in range(2)]
    rg = P.sbuf([128, T], F32, name="rg")
    sqb = [get_tmp(P, C, f"sq{i}", [128, 512], F32) for i in range(2)]
    ny = 0
    for grp in range(2):
        yr = yd[grp].rearrange("(kt p) t -> p kt t", p=128)
        nsq = 0
        for kt in range(8):
            ys = yst[ny % 2]
            yk = ('yst', ny % 2)
            ny += 1
            P.dma('sp', ys[:], yr[:, kt, :], writes=[yk])
            for ci, (t0, n) in enumerate(CHUNKS):
                sq = sqb[nsq % 2]
                sk_ = ('sq', nsq % 2)
                nsq += 1
                act(P, sq[:, :n], ys[:, t0:t0 + n], AF.Square, reads=[yk], writes=[sk_])
                mm(P, B[4 + ci][:, :n], C.ones[:], sq[:, :n], kt == 0, kt == 7, reads=[sk_, 'ones'], writes=[('bank', 4 + ci)])
        for ci, (t0, n) in enumerate(CHUNKS):
            act(P, rg[:, t0:t0 + n], B[4 + ci][:, :n], AF.Sqrt, reads=[('bank', 4 + ci), 'epsc'], writes=['rg'], bias=C.eps[:, 0:1], scale=1.0 / HY)
        P.op('dve', lambda e: e.reciprocal(out=rg[:], in_=rg[:]), reads=['rg'], writes=['rg'])
        for kt in range(8):
            ys = yst[ny % 2]
            yk = ('yst', ny % 2)
            ny += 1
            P.dma('sp', ys[:], yr[:, kt, :], writes=[yk])
            stt(P, hT[:, grp * 8 + kt, :], ys[:], gg[grp][:, kt:kt + 1], rg[:], ALU.mult, ALU.mult, reads=[yk, ('gg', grp), 'rg'], writes=hkeys)

    wg = [P.sbuf([128, KT, 128], BF16, name=f"wg{i}") for i in range(2)]
    wu = [P.sbuf([128, KT, 128], BF16, name=f"wu{i}") for i in range(2)]

    def gate_ap(ci, row, ct):
        cls = 'm' if ci < 2 else 'c'
        return modt[cls][:, row * KT + ct:row * KT + ct + 1], ('mod', cls)

    for ct in range(KT):
        P.dma('pool', wg[ct % 2][:], wod[ct], writes=[('wg', ct % 2)])
        w = wg[ct % 2]
        bset = (ct % 2) * 3
        for kt in range(KT):
            for ci, (t0, n) in enumerate(CHUNKS):
                mm(P, B[bset + ci][:, :n], w[:, kt, :], hT[:, kt, t0:t0 + n], kt == 0, kt == KT - 1,
                   reads=[('wg', ct % 2), ('h', ci)], writes=[('bank', bset + ci)])
        for ci, (t0, n) in enumerate(CHUNKS):
            ga, gk = gate_ap(ci, 2, ct)
            stt(P, xT[:, ct, t0:t0 + n], B[bset + ci][:, :n], ga, xT[:, ct, t0:t0 + n], ALU.mult, ALU.add,
                reads=[('bank', bset + ci), ('x', ci), gk], writes=[('x', ci)])

    emit_adaln(P, C, xT, hT, modt, gf, 3, 4, 'ffn')
    aT = [P.sbuf([128, GF, T], BF16, name="aT0")] * 2
    wdb = [P.sbuf([128, GF, 256], BF16, name=f"wdb{i}") for i in range(2)]
    sgt = [P.sbuf([128, 512], F32, name=f"sgt{i}") for i in range(2)]
    nsg = 0
    nwd = 0
    for gi in range(NGRP):
        a_ = aT[gi % 2]
        for fl in range(GF):
            f = gi * GF + fl
            P.dma('pool', wg[f % 2][:], wgd[f], writes=[('wg', f % 2)])
            P.dma('pool', wu[f % 2][:], wud[f], writes=[('wu', f % 2)])
            for ci, (t0, n) in enumerate(CHUNKS):
                bG, bU = nsg % 2, 2 + nsg % 2
                s_ = sgt[nsg % 2]
                sk_ = ('sgt', nsg % 2)
                nsg += 1
                for kt in range(KT):
                    mm(P, B[bG][:, :n], wg[f % 2][:, kt, :], hT[:, kt, t0:t0 + n], kt == 0, kt == KT - 1, reads=[('wg', f % 2), ('h', ci)], writes=[('bank', bG)])
                for kt in range(KT):
                    mm(P, B[bU][:, :n], wu[f % 2][:, kt, :], hT[:, kt, t0:t0 + n], kt == 0, kt == KT - 1, reads=[('wu', f % 2), ('h', ci)], writes=[('bank', bU)])
                act(P, s_[:, :n], B[bG][:, :n], AF.Silu, reads=[('bank', bG)], writes=[sk_])
                tt(P, 'dve', a_[:, fl, t0:t0 + n], s_[:, :n], B[bU][:, :n], ALU.mult, reads=[sk_, ('bank', bU)], writes=[('aT', ci)])
        for blk in range(D // 256):
            wd_ = wdb[nwd % 2]
            wk = ('wdb', nwd % 2)
            nwd += 1
            P.dma('pool', wd_[:], wdd[gi, blk], writes=[wk])
            for sub in range(2):
                ct = 2 * blk + sub
                bset = (ct % 2) * 3
                for fl in range(GF):
                    for ci, (t0, n) in enumerate(CHUNKS):
                        mm(P, B[bset + ci][:, :n], wd_[:, fl, sub * 128:(sub + 1) * 128], a_[:, fl, t0:t0 + n], fl == 0, fl == GF - 1,
                           reads=[wk, ('aT', ci)], writes=[('bank', bset + ci)])
                for ci, (t0, n) in enumerate(CHUNKS):
                    ga, gk = gate_ap(ci, 5, ct)
                    stt(P, xT[:, ct, t0:t0 + n], B[bset + ci][:, :n], ga, xT[:, ct, t0:t0 + n], ALU.mult, ALU.add,
                        reads=[('bank', bset + ci), ('x', ci), gk], writes=[('x', ci)])
    xor_ = xod.rearrange("(kt p) t -> p kt t", p=128)
    for ci, (t0, n) in enumerate(CHUNKS):
        P.dma('sp', xor_[:, :, t0:t0 + n], xT[:, :, t0:t0 + n], reads=[('x', ci)], is_output=True)
    if final:
        gfin = P.sbuf([128, KT], F32, name="gfin")
        P.dma('sp', gfin[:], gfin_d, writes=['gfin'])
        rstd = get_tmp(P, C, "rstd", [128, 512], F32)
        hnb = [get_tmp(P, C, f"hn{i}", [128, 512], F32) for i in range(2)]
        fo = [P.sbuf([128, 512], F32, name=f"fo{i}") for i in range(2)]
        nf = 0
        for ci, (t0, n) in enumerate(CHUNKS[:2]):
            emit_rstd(P, C, lambda kt: xT[:, kt, t0:t0 + n], KT, D, ci, t0, n, rstd, ('x', ci))
            for kt in range(KT):
                hn = hnb[kt % 2]
                f_ = fo[nf % 2]
                fk = ('fo', nf % 2)
                nf += 1
                tt(P, 'dve', hn[:, :n], xT[:, kt, t0:t0 + n], rstd[:, :n], ALU.mult, reads=[('x', ci), 'rstd'], writes=[('hn', kt % 2)])
                act(P, f_[:, :n], hn[:, :n], AF.Identity, reads=[('hn', kt % 2), 'gfin'], writes=[fk], scale=gfin[:, kt:kt + 1])
                P.dma('sp', outd[kt * 128:(kt + 1) * 128, t0:t0 + n], f_[:, :n], reads=[fk], is_output=True)
    return P.finalize()


MCOLS = 6 * D // NCORES


def build_m():
    P = Prog()
    C = Ctx(P)
    I = "ExternalInput"
    c3d = P.dram("c3T", [D, 3], F32, I)
    wmd = P.dram("wm", [DEPTH, MCOLS // 512, 128, KT, 512], F32, I)
    bmd = P.dram("bm", [3, DEPTH, MCOLS], F32, I)
    mod_o = P.dram("modo", [3, DEPTH, MCOLS], F32, "ExternalOutput")
    sc = P.sbuf([128, KT, 3], F32, name="sc")
    bmt = P.sbuf([3, DEPTH, MCOLS], F32, name="bmt")
    P.dma('sp', sc[:], c3d.rearrange("(kt p) c -> p kt c", p=128), writes=['sc'])
    P.dma('sp', bmt[:], bmd, writes=['bmt'])
    act(P, sc[:], sc[:], AF.Silu, reads=['sc'], writes=['sc'])
    wmb = [P.sbuf([128, KT, 512], F32, name=f"wmb{i}") for i in range(2)]
    ost = [P.sbuf([3, 512], F32, name=f"most{i}") for i in range(2)]
    B = C.banks
    k = 0
    for l in range(DEPTH):
        for blk in range(MCOLS // 512):
            w = wmb[k % 2]
            P.dma('sp', w[:], wmd[l, blk], writes=[('wmb', k % 2)])
            for kt in range(KT):
                mm(P, B[k % 2][0:3, :512], sc[:, kt, :], w[:, kt, :], kt == 0, kt == KT - 1, reads=['sc', ('wmb', k % 2)], writes=[('bank', k % 2)])
            tt(P, 'dve', ost[k % 2][:], B[k % 2][0:3, :512], bmt[:, l, blk * 512:(blk + 1) * 512], ALU.add, reads=[('bank', k % 2), 'bmt'], writes=[('most', k % 2)])
            P.dma('pool', mod_o[:, l, blk * 512:(blk + 1) * 512], ost[k % 2][:], reads=[('most', k % 2)], is_output=True)
            k += 1
    return P.finalize()


def h_consts():
    import ml_dtypes
    tabm, wfm = dft_tables(L)
    tabc, wfc = dft_tables(CTX)
    embm, winm = filt_consts(L)
    embc, winc = filt_consts(CTX)
    idn = np.eye(128, dtype=np.float32).astype(ml_dtypes.bfloat16)
    return dict(tabm=tabm, wfm=wfm, tabc=tabc, wfc=wfc, embm=embm, winm=winm, embc=embc, winc=winc, idn=idn)


def host_h_inputs(i, inp, uT_full, ucT_full, consts):
    maps = []
    for r in range(NCORES):
        c0 = 128 * r
        um = np.stack([uT_full[g * HY + c0:g * HY + c0 + 128] for g in range(3)], axis=1)
        uc = np.stack([ucT_full[g * HY + c0:g * HY + c0 + 128] for g in range(3)], axis=1)
        um = np.pad(um, ((0, 0), (0, 0), (0, 0), (1, 1)))
        uc = np.pad(uc, ((0, 0), (0, 0), (0, 0), (1, 1)))
        cw = np.stack([inp['conv_w'][i][tap, g * HY + c0:g * HY + c0 + 128] for g in range(3) for tap in range(3)], axis=1)
        cb = np.stack([inp['conv_b'][i][g * HY + c0:g * HY + c0 + 128] for g in range(3)], axis=1)
        w3 = inp['filt_w3'][i].reshape(64, 4, HY)[:, :, c0:c0 + 128]
        fb = np.broadcast_to(inp['filt_bias'][i][None, :, c0:c0 + 128], (128, 2, 128))
        A = np.ascontiguousarray
        maps.append(dict(um=A(um), uc=A(uc), cw=A(cw), cb=A(cb),
                         w1=A(inp['filt_w1'][i]), b1=A(inp['filt_b1'][i][:, None]),
                         w2=A(inp['filt_w2'][i]), b2=A(inp['filt_b2'][i][:, None]),
                         fq=A(inp['filt_freq'][i][:, None]), w3=A(w3), fb=A(fb),
                         idn=consts['idn'], embm=consts['embm'], embc=consts['embc'],
                         winm=A(consts['winm'][c0:c0 + 128]), winc=A(consts['winc'][c0:c0 + 128]),
                         tabm=consts['tabm'], tabc=consts['tabc'], wfm=consts['wfm'], wfc=consts['wfc']))
    return maps


def wdown_blocks(w):
    return np.ascontiguousarray(w.reshape(NGRP, GF, 128, D // 256, 256).transpose(0, 3, 2, 1, 4))


def split_tokens(full, fullc):
    out = []
    for r in range(NCORES):
        b, j = r // 4, r % 4
        out.append(np.ascontiguousarray(np.concatenate([full[:, b, j * TM:(j + 1) * TM], fullc[:, b, j * TC:(j + 1) * TC]], axis=1)))
    return out


def gather_tokens(per_core, key):
    F = per_core[0][key].shape[0]
    dt = per_core[0][key].dtype
    full = np.zeros((F, 2, L), dt)
    fullc = np.zeros((F, 2, CTX), dt)
    for r in range(NCORES):
        b, j = r // 4, r % 4
        a = per_core[r][key]
        full[:, b, j * TM:(j + 1) * TM] = a[:, :TM]
        fullc[:, b, j * TC:(j + 1) * TC] = a[:, TM:]
    return full, fullc


_PROGS = {}


def prog(name):
    if name not in _PROGS:
        _PROGS[name] = dict(m=build_m, p1=build_p1, h=build_h, a=build_a, p2=lambda: build_p2(False), p2f=lambda: build_p2(True))[name]()
    return _PROGS[name]


def run(name, maps):
    return run_bass_kernel_spmd(prog(name), maps, core_ids=list(range(NCORES))).results


def kernel(**inputs):
    inp = {k: np.asarray(v) for k, v in inputs.items()}
    A = np.ascontiguousarray
    x, c, ctx, c_ctx = inp['x'], inp['c'], inp['ctx'], inp['c_ctx']
    c3T = A(np.stack([c[0], c[1], c_ctx], axis=1).astype(np.float32))
    maps = []
    for r in range(NCORES):
        wm = inp['w_mod'][:, :, r * MCOLS:(r + 1) * MCOLS].reshape(DEPTH, KT, 128, MCOLS // 512, 512).transpose(0, 3, 2, 1, 4)
        bm = np.broadcast_to(inp['b_mod'][None, :, r * MCOLS:(r + 1) * MCOLS], (3, DEPTH, MCOLS))
        maps.append(dict(c3T=c3T, wm=A(wm), bm=A(bm)))
    res = run('m', maps)
    mod_full = np.concatenate([res[r]['modo'] for r in range(NCORES)], axis=2)
    del maps

    consts = h_consts()
    cos, sin = rope_tables()
    pm = perm_matrix()
    cosT, sinT = [], []
    for r in range(NCORES):
        j = r % 4
        cosT.append(A(np.concatenate([cos[:, j * TM:(j + 1) * TM], np.ones((128, TC), np.float32)], axis=1)))
        sinT.append(A(np.concatenate([sin[:, j * TM:(j + 1) * TM], np.zeros((128, TC), np.float32)], axis=1)))
    xT = []
    for r in range(NCORES):
        b, j = r // 4, r % 4
        xT.append(A(np.concatenate([x[b, j * TM:(j + 1) * TM].T, ctx[b, j * TC:(j + 1) * TC].T], axis=1)))
    out = None
    for i in range(DEPTH):
        modm = [vecT(mod_full[b, i]) for b in range(2)]
        modc = vecT(mod_full[2, i])
        wblk = w_blocks(inp['w_in'][i], 256)
        gm = vecT(inp['norm_mix_g'][i])
        maps = [dict(xT=xT[r], modm=modm[r // 4], modc=modc, gmix=gm, w_in=wblk, cosT=cosT[r], sinT=sinT[r], perm=pm) for r in range(NCORES)]
        res = run('p1', maps)
        del maps, wblk
        uT_full, ucT_full = gather_tokens(res, 'uT')
        qkvT_full, qkvcT_full = gather_tokens(res, 'qkvT')
        res = run('h', host_h_inputs(i, inp, uT_full, ucT_full, consts))
        yhy = np.concatenate([res[r]['yhm'] for r in range(NCORES)], axis=0)
        yhyc = np.concatenate([res[r]['yhc'] for r in range(NCORES)], axis=0)
        res = run('a', host_a_inputs(i, inp['attn_sink'], qkvT_full, qkvcT_full))
        yat, yatc = host_a_gather(res)
        final = i == DEPTH - 1
        yh_c = split_tokens(yhy, yhyc)
        ya_c = split_tokens(yat, yatc)
        wo = w_blocks(inp['w_out'][i], 128)
        wg = w_blocks(inp['w_gate'][i], 128)
        wu = w_blocks(inp['w_up'][i], 128)
        wd = wdown_blocks(inp['w_down'][i])
        ghy = A(inp['out_norm_hy'][i].reshape(8, 128).T)
        gat = A(inp['out_norm_att'][i].reshape(8, 128).T)
        gffn = vecT(inp['norm_ffn_g'][i])
        maps = []
        for r in range(NCORES):
            m = dict(xT=xT[r], yhyT=yh_c[r], yatT=ya_c[r], modm=modm[r // 4], modc=modc, ghy=ghy, gat=gat, gffn=gffn,
                     w_out=wo, w_gate=wg, w_up=wu, w_down=wd)
            if final:
                m['gfin'] = vecT(inp['final_g'])
            maps.append(m)
        res = run('p2f' if final else 'p2', maps)
        del maps, wo, wg, wu, wd
        xT = [res[r]['xTo'] for r in range(NCORES)]
        if final:
            out = np.zeros((2, L, D), np.float32)
            for r in range(NCORES):
                b, j = r // 4, r % 4
                out[b, j * TM:(j + 1) * TM, :] = res[r]['outT'].T
    return out
```
